# Optimizing a Trainium2 kernel written in Bass

```python
import math
import jax, jax.numpy as jnp
from jax import lax
import numpy as np

D_MODEL = 1024
BATCH = 16
SEQ = 2048
DEPTH = 2

CHUNK = 64
Q_BLOCK = 128
N_MIXERS = 2
N_A_LAYERS = (DEPTH + 1) // 2
N_B_LAYERS = DEPTH // 2
FOX_HEADS = 16
FOX_HEAD_DIM = D_MODEL // FOX_HEADS
GLA_HEADS = 4
GLA_DK = D_MODEL // 2
GLA_DV = D_MODEL
GLA_HK = GLA_DK // GLA_HEADS
GLA_HV = GLA_DV // GLA_HEADS
GLA_RANK = 16
GLA_TAU = 16.0
FFN_HIDDEN = int(math.ceil(8 * D_MODEL / 3 / 256) * 256)
N_MOD = 6
EPS = 1e-6

kernel_name = "fox_gla_adaln_hybrid_trunk"


def rms_norm(x, g):
    xf = x.astype(jnp.float32)
    y = xf * lax.rsqrt(jnp.mean(xf * xf, axis=-1, keepdims=True) + EPS)
    return (y * g.astype(jnp.float32)).astype(x.dtype)


def modulate(h, shift, scale):
    return h * (1.0 + scale[:, None, :]) + shift[:, None, :]


def forgetting_attention(h, w_in, b_f, w_out):
    B, S, _ = h.shape
    proj = h @ w_in
    q, k, v, f_logit = jnp.split(proj, [D_MODEL, 2 * D_MODEL, 3 * D_MODEL], axis=-1)
    q = q.reshape(B, S, FOX_HEADS, FOX_HEAD_DIM)
    k = k.reshape(B, S, FOX_HEADS, FOX_HEAD_DIM)
    v = v.reshape(B, S, FOX_HEADS, FOX_HEAD_DIM)
    log_f = jax.nn.log_sigmoid(f_logit.astype(jnp.float32) + b_f.astype(jnp.float32))
    cum = jnp.cumsum(log_f, axis=1).transpose(0, 2, 1)
    scale = FOX_HEAD_DIM ** -0.5
    outs = []
    for blk in range(S // Q_BLOCK):
        qs, qe = blk * Q_BLOCK, (blk + 1) * Q_BLOCK
        logits = jnp.einsum('bqhd,bkhd->bhqk', q[:, qs:qe], k[:, :qe]).astype(jnp.float32) * scale
        logits = logits + cum[:, :, qs:qe, None] - cum[:, :, None, :qe]
        mask = (qs + jnp.arange(Q_BLOCK))[:, None] >= jnp.arange(qe)[None, :]
        logits = jnp.where(mask[None, None], logits, -jnp.inf)
        p = jax.nn.softmax(logits, axis=-1).astype(v.dtype)
        outs.append(jnp.einsum('bhqk,bkhd->bqhd', p, v[:, :qe]))
    o = jnp.concatenate(outs, axis=1).reshape(B, S, D_MODEL)
    return o @ w_out


def gated_linear_attention(h, w_in, w_a2, b_a, g_o, w_out):
    B, S, _ = h.shape
    N = S // CHUNK
    proj = h @ w_in
    q, k, v, r, a_lr = jnp.split(
        proj, [GLA_DK, 2 * GLA_DK, 2 * GLA_DK + GLA_DV, 2 * GLA_DK + 2 * GLA_DV], axis=-1)
    log_alpha = jax.nn.log_sigmoid((a_lr @ w_a2 + b_a).astype(jnp.float32)) / GLA_TAU
    q = (q.astype(jnp.float32) * GLA_HK ** -0.5).reshape(B, N, CHUNK, GLA_HEADS, GLA_HK)
    k = k.astype(jnp.float32).reshape(B, N, CHUNK, GLA_HEADS, GLA_HK)
    v = v.astype(jnp.float32).reshape(B, N, CHUNK, GLA_HEADS, GLA_HV)
    b = jnp.cumsum(log_alpha.reshape(B, N, CHUNK, GLA_HEADS, GLA_HK), axis=2)
    b_last = b[:, :, -1]
    q_in = q * jnp.exp(b)
    k_in = k * jnp.exp(-b)
    A = jnp.einsum('bnthk,bnshk->bnhts', q_in, k_in)
    tril = jnp.tril(jnp.ones((CHUNK, CHUNK), dtype=bool))
    A = jnp.where(tril, A, 0.0)
    o_intra = jnp.einsum('bnhts,bnshv->bnthv', A, v)
    k_dec = k * jnp.exp(b_last[:, :, None] - b)
    kv = jnp.einsum('bnshk,bnshv->bnhkv', k_dec, v)

    def step(state, xs):
        dec, kv_n = xs
        return dec[..., None] * state + kv_n, state

    state0 = jnp.zeros((B, GLA_HEADS, GLA_HK, GLA_HV), jnp.float32)
    _, s_prev = lax.scan(step, state0, (jnp.moveaxis(jnp.exp(b_last), 1, 0), jnp.moveaxis(kv, 1, 0)))
    o_inter = jnp.einsum('bnthk,nbhkv->bnthv', q_in, s_prev)
    o = (o_intra + o_inter).reshape(B, S, GLA_HEADS, GLA_HV)
    o = o * lax.rsqrt(jnp.mean(o * o, axis=-1, keepdims=True) + EPS)
    o = o.reshape(B, S, GLA_DV) * g_o.astype(jnp.float32)
    o = (o * jax.nn.silu(r.astype(jnp.float32))).astype(h.dtype)
    return o @ w_out


def swiglu(h, w_in, w_out):
    gate, up = jnp.split(h @ w_in, 2, axis=-1)
    return (jax.nn.silu(gate) * up) @ w_out


def setup_inputs(seed: int = 0) -> dict:
    key = jax.random.key(seed)
    ks = jax.random.split(key, 20)
    f32 = jnp.float32

    def nrm(k, shape, fan_in, mult=1.0):
        return jax.random.normal(k, shape, f32) * (mult * fan_in ** -0.5)

    D = D_MODEL
    fox_cols = 3 * D + FOX_HEADS
    gla_cols = 2 * GLA_DK + 2 * GLA_DV + GLA_RANK
    return {
        "x": jax.random.normal(ks[0], (BATCH, SEQ, D), f32),
        "c": jax.random.normal(ks[1], (BATCH, D), f32),
        "ada_w": nrm(ks[2], (DEPTH, D, N_MOD * D), D, 0.5),
        "ada_b": 0.01 * jax.random.normal(ks[3], (DEPTH, N_MOD * D), f32),
        "norm1_g": 1.0 + 0.02 * jax.random.normal(ks[4], (DEPTH, D), f32),
        "norm2_g": 1.0 + 0.02 * jax.random.normal(ks[5], (DEPTH, D), f32),
        "ffn_w_in": nrm(ks[6], (DEPTH, D, 2 * FFN_HIDDEN), D),
        "ffn_w_out": nrm(ks[7], (DEPTH, FFN_HIDDEN, D), FFN_HIDDEN),
        "fox_w_in": nrm(ks[8], (N_A_LAYERS, D, fox_cols), D),
        "fox_b_f": jax.random.uniform(ks[9], (N_A_LAYERS, FOX_HEADS), f32, 1.0, 4.0),
        "fox_w_out": nrm(ks[10], (N_A_LAYERS, D, D), D),
        "gla_w_in": nrm(ks[11], (N_B_LAYERS, D, gla_cols), D),
        "gla_w_a2": nrm(ks[12], (N_B_LAYERS, GLA_RANK, GLA_DK), GLA_RANK),
        "gla_b_a": 0.1 * jax.random.normal(ks[13], (N_B_LAYERS, GLA_DK), f32),
        "gla_g_o": 1.0 + 0.02 * jax.random.normal(ks[14], (N_B_LAYERS, GLA_DV), f32),
        "gla_w_out": nrm(ks[15], (N_B_LAYERS, GLA_DV, D), GLA_DV),
        "final_g": 1.0 + 0.02 * jax.random.normal(ks[16], (D,), f32),
    }


def reference(x, c, ada_w, ada_b, norm1_g, norm2_g, ffn_w_in, ffn_w_out,
              fox_w_in, fox_b_f, fox_w_out,
              gla_w_in, gla_w_a2, gla_b_a, gla_g_o, gla_w_out, final_g):
    c_act = jax.nn.silu(c)
    for i in range(DEPTH):
        mod = c_act @ ada_w[i] + ada_b[i]
        sh1, sc1, g1, sh2, sc2, g2 = jnp.split(mod, N_MOD, axis=-1)
        h = modulate(rms_norm(x, norm1_g[i]), sh1, sc1)
        j = i // N_MIXERS
        if i % N_MIXERS == 0:
            y = forgetting_attention(h, fox_w_in[j], fox_b_f[j], fox_w_out[j])
        else:
            y = gated_linear_attention(h, gla_w_in[j], gla_w_a2[j], gla_b_a[j], gla_g_o[j], gla_w_out[j])
        x = x + g1[:, None, :] * y
        h = modulate(rms_norm(x, norm2_g[i]), sh2, sc2)
        x = x + g2[:, None, :] * swiglu(h, ffn_w_in[i], ffn_w_out[i])
    return rms_norm(x, final_g)
```

```python
import numpy as np
import concourse.bass as bass
import concourse.mybir as mybir
from concourse.bass_utils import run_bass_kernel_spmd

F32 = mybir.dt.float32
BF16 = mybir.dt.bfloat16
AF = mybir.ActivationFunctionType
ALU = mybir.AluOpType

D = 1024
KC = 8
NH_FOX = 16
FFN_H = 2816
NCH = FFN_H // 128
EPS = 1e-6
N_CORES = 8


class _Op:
    __slots__ = ("eng", "fn", "deps", "signal", "token", "dma", "waits", "idx")

    def __init__(self, eng, fn, dma):
        self.eng = eng
        self.fn = fn
        self.deps = set()
        self.signal = False
        self.token = None
        self.dma = dma
        self.waits = []


class _Rec:
    def __init__(self):
        self.call = None

    def __getattr__(self, name):
        def f(*a, **k):
            self.call = (name, a, k)
            return self
        return f


class Sched:
    ENGS = ["pe", "act", "dve", "pool", "sp"]

    def __init__(self, ring=8):
        self.ops = []
        self.by_eng = {e: [] for e in self.ENGS}
        self.last_w = {}
        self.readers = {}
        self.ring = ring
        self.dma_count = {"sp": 0, "pool": 0}
        self.dma_ring_last = {"sp": [None] * ring, "pool": [None] * ring}
        self.dma_ring_cnt = {"sp": [0] * ring, "pool": [0] * ring}
        self.pending_dma = []

    def op(self, eng, fn, r=(), w=(), dma=False):
        rec = _Rec()
        fn(rec)
        call = rec.call
        fn = lambda e, call=call: getattr(e, call[0])(*call[1], **call[2])
        o = _Op(eng, fn, dma)
        o.idx = len(self.ops)
        self.ops.append(o)
        self.by_eng[eng].append(o)
        deps = set()
        for k in r:
            lw = self.last_w.get(k)
            if lw is not None:
                deps.add(lw)
        for k in w:
            lw = self.last_w.get(k)
            if lw is not None:
                deps.add(lw)
            rd = self.readers.get(k)
            if rd:
                deps.update(rd.values())
        for k in r:
            rd = self.readers.setdefault(k, {})
            if dma:
                rd[("dma", o.idx)] = o.idx
            else:
                rd[eng] = o.idx
        for k in w:
            self.last_w[k] = o.idx
            self.readers[k] = {}
        if dma:
            slot = self.dma_count[eng] % self.ring
            self.dma_count[eng] += 1
            prev = self.dma_ring_last[eng][slot]
            if prev is not None:
                deps.add(prev)
            self.dma_ring_last[eng][slot] = o.idx
            self.dma_ring_cnt[eng][slot] += 1
            o.token = ((eng, slot), 16 * self.dma_ring_cnt[eng][slot])
            o.signal = True
            self.pending_dma.append(o.idx)
        deps.discard(o.idx)
        o.deps = deps
        return o.idx

    def barrier(self):
        marks = []
        for e in ["pe", "act", "dve", "pool"]:
            lst = [x for x in self.by_eng[e] if not x.dma and x.fn is not None]
            if lst:
                marks.append(lst[-1].idx)
        dmas = list(self.pending_dma)
        for e in self.ENGS:
            o = _Op(e, None, False)
            o.idx = len(self.ops)
            self.ops.append(o)
            self.by_eng[e].append(o)
            o.deps = set(marks) | set(dmas)
        self.pending_dma = []
        self.last_w = {}
        self.readers = {}

    def finalize(self):
        for o in self.ops:
            for d in o.deps:
                p = self.ops[d]
                if p.dma:
                    continue
                if p.eng == o.eng and not o.dma:
                    continue
                p.signal = True
        for e in self.ENGS:
            cnt = 0
            for o in self.by_eng[e]:
                if o.dma:
                    continue
                if o.signal:
                    cnt += 1
                    o.token = (e, cnt)
        for e in self.ENGS:
            known = {}
            for o in self.by_eng[e]:
                need = {}
                for d in o.deps:
                    p = self.ops[d]
                    if (not p.dma) and p.eng == o.eng and not o.dma:
                        continue
                    sem, val = p.token
                    if val > need.get(sem, 0):
                        need[sem] = val
                o.waits = []
                for sem, val in need.items():
                    if known.get(sem, 0) >= val:
                        continue
                    known[sem] = val
                    o.waits.append((sem, val))

    def emit_engine(self, e, eng, sems):
        for o in self.by_eng[e]:
            for sem, val in o.waits:
                eng.wait_ge(sems[sem], val)
            if o.fn is None:
                continue
            ins = o.fn(eng)
            if o.signal:
                sem, val = o.token
                ins.then_inc(sems[sem], 16 if o.dma else 1)


class Arena:
    def __init__(self, nc, limit):
        self.nc = nc
        self.off = 16640
        self.limit = limit
        self.n = 0

    def alloc(self, name, cols, dt):
        nbytes = cols * (4 if dt == F32 else 2)
        nbytes = (nbytes + 63) // 64 * 64
        off = self.off
        self.off += nbytes
        assert self.off <= self.limit, f"SBUF arena overflow at {name}: {self.off}"
        self.n += 1
        return self.nc.alloc_sbuf_tensor_at(f"{name}_{self.n}", [128, cols], dt, offset=off)

    def mark(self):
        return self.off

    def reset(self, m):
        self.off = m


def build_program(S=2048, nseq=2, depth=2, dbg=False, plan=None):
    NTB = S // 512
    NTT = S // 128
    nc = bass.Bass("TRN2", target_bir_lowering=False)
    sc = Sched(ring=8)

    def din(name, shape):
        return nc.dram_tensor(name, list(shape), F32, kind="ExternalInput").ap()

    d_xT = din("xT", [nseq, 128, KC, S])
    d_cT = din("cT", [128, KC * nseq])
    d_adaw = din("adaw", [2, 6, 128, 8 * KC * 128])
    d_adab = din("adab", [128, 96])
    d_n1g = din("n1g", [128, 16])
    d_n2g = din("n2g", [128, 16])
    d_fg = din("fg", [128, 8])
    d_ffn_in = din("ffn_in", [2, NCH, 128, KC * 256])
    d_ffn_out = din("ffn_out", [2, NCH, 128, D])
    d_fox_in = din("fox_in", [8, 128, KC * 384])
    d_fox_f = din("fox_f", [128, KC * 16])
    d_fox_bf = din("fox_bf", [16, 1])
    d_fox_out = din("fox_out", [8, 128, D])
    d_gla_in = din("gla_in", [4, 128, KC * 768])
    d_gla_a = din("gla_a", [128, KC * 16])
    d_gla_a2 = din("gla_a2", [17, 512])
    d_gla_go = din("gla_go", [128, 8])
    d_gla_out = din("gla_out", [4, 128, 2 * D])
    d_out = nc.dram_tensor("outT", [nseq, 128, KC, S], F32, kind="ExternalOutput").ap()

    d_dbg = nc.dram_tensor("dbg", [8, 128, 512], F32, kind="ExternalOutput").ap() if dbg else None
    dbg_done = set()

    def dump(i, ap, rkeys, np_=128, ncol=512):
        if not dbg or i in dbg_done:
            return
        dbg_done.add(i)
        sc.op("pool", lambda e: e.dma_start(out=d_dbg[i, 0:np_, 0:ncol], in_=ap), r=rkeys, w=[("dbg", i)], dma=True)

    ar = Arena(nc, 229376)
    A = ar.alloc

    def v3(t, b):
        return t[:].rearrange("p (a b) -> p a b", b=b)

    xT = A("xT", KC * S, F32)
    hT = A("hT", KC * S, BF16)
    xTv = v3(xT, S)
    hTv = v3(hT, S)
    ident_f = A("ident_f", 128, F32)
    ones_b = A("ones_b", 128, BF16)
    ident_b = A("ident_b", 128, BF16)
    negmask = A("negmask", 128, BF16)
    tri_incl = A("tri_incl", 128, F32)
    tri_after = A("tri_after", 128, F32)
    tri4 = A("tri4", 512, F32)
    eps_t = A("eps_t", 1, F32)
    one_t = A("one_t", 1, F32)
    cT = A("cT", KC * nseq, F32)
    cact = A("cact", KC * nseq, BF16)
    adab = A("adab", 96, F32)
    n1g = A("n1g", 16, F32)
    n2g = A("n2g", 16, F32)
    fg = A("fg", 8, F32)
    gla_go = A("gla_go", 8, F32)
    fox_nbf = A("fox_nbf", 1, F32)
    gla_a2 = A("gla_a2", 512, BF16)
    modv = [A(f"modv{s}", 96, F32) for s in range(nseq)]
    a1 = [[A(f"a1_{s}_{l}", 8, F32) for l in range(2)] for s in range(nseq)]
    a2 = [[A(f"a2_{s}_{l}", 8, F32) for l in range(2)] for s in range(nseq)]
    scratch = A("scratch", 16, F32)
    PBASE = ar.mark()

    psb = [nc.alloc_psum_tensor(f"ps{i}", [128, 512], F32) for i in range(8)]
    PS = [f"ps{i}" for i in range(8)]

    def modslice(s, l, which):
        base = l * 48 + which * 8
        return modv[s][:, base:base + 8]

    sc.op("pool", lambda e: e.memset(ident_f[:], 0.0), w=["ident_f"])
    sc.op("pool", lambda e: e.affine_select(out=ident_f[:], in_=ident_f[:], pattern=[[-1, 128]],
                                            compare_op=ALU.not_equal, fill=1.0, base=0, channel_multiplier=1),
          r=["ident_f"], w=["ident_f"])
    sc.op("pool", lambda e: e.tensor_copy(out=ident_b[:], in_=ident_f[:]), r=["ident_f"], w=["ident_b"])
    sc.op("pool", lambda e: e.memset(ones_b[:], 1.0), w=["ones_b"])
    sc.op("pool", lambda e: e.memset(negmask[:], 0.0), w=["negmask"])
    sc.op("pool", lambda e: e.affine_select(out=negmask[:], in_=negmask[:], pattern=[[1, 128]],
                                            compare_op=ALU.is_ge, fill=-30000.0, base=0, channel_multiplier=-1),
          r=["negmask"], w=["negmask"])
    sc.op("pool", lambda e: e.memset(tri_incl[:], 1.0), w=["tri_incl"])
    sc.op("pool", lambda e: e.affine_select(out=tri_incl[:], in_=tri_incl[:], pattern=[[1, 128]],
                                            compare_op=ALU.is_ge, fill=0.0, base=0, channel_multiplier=-1),
          r=["tri_incl"], w=["tri_incl"])
    sc.op("pool", lambda e: e.memset(tri_incl[0:64, 64:128], 0.0), w=["tri_incl"])
    sc.op("pool", lambda e: e.memset(tri_after[:], 1.0), w=["tri_after"])
    sc.op("pool", lambda e: e.affine_select(out=tri_after[:], in_=tri_after[:], pattern=[[-1, 128]],
                                            compare_op=ALU.is_ge, fill=0.0, base=-1, channel_multiplier=1),
          r=["tri_after"], w=["tri_after"])
    sc.op("pool", lambda e: e.memset(tri_after[64:128, 0:64], 0.0), w=["tri_after"])
    for i in range(4):
        sc.op("pool", lambda e, i=i: e.tensor_copy(out=tri4[:, 128 * i:128 * i + 128], in_=tri_incl[:]),
              r=["tri_incl"], w=["tri4"])
    sc.op("pool", lambda e: e.memset(eps_t[:], EPS), w=["eps_t"])
    sc.op("pool", lambda e: e.memset(one_t[:], 1.0), w=["one_t"])

    def ld(dst_ap, src_ap, wkeys, eng="sp"):
        return sc.op(eng, lambda e: e.dma_start(out=dst_ap, in_=src_ap), w=wkeys, dma=True)

    ld(cT[:], d_cT[:, :], ["cT"])
    ld(adab[:], d_adab[:, :], ["adab"])
    ld(n1g[:], d_n1g[:, :], ["n1g"])
    ld(n2g[:], d_n2g[:, :], ["n2g"])
    ld(fg[:], d_fg[:, :], ["fg"])
    ld(gla_go[:], d_gla_go[:, :], ["gla_go"])
    ld(fox_nbf[0:16, :], d_fox_bf[:, :], ["fox_nbf"])
    ld(gla_a2[0:17, :], d_gla_a2[:, :], ["gla_a2"], eng="pool")
    sc.op("dve", lambda e: e.tensor_scalar_mul(out=fox_nbf[0:16, :], in0=fox_nbf[0:16, :], scalar1=-1.0),
          r=["fox_nbf"], w=["fox_nbf"])
    sc.op("act", lambda e: e.activation(out=cact[:], in_=cT[:], func=AF.Silu), r=["cT"], w=["cact"])

    m0 = ar.mark()
    adaw_buf = [A(f"adaw{i}", 8 * KC * 128, BF16) for i in range(2)]
    cactv = v3(cact, nseq)
    for l in range(2):
        for g in range(6):
            bi = (l * 6 + g) % 2
            buf = adaw_buf[bi]
            ld(buf[:], d_adaw[l, g, :, :], [f"adaw{bi}"], eng="pool")
            bv = buf[:].rearrange("p (f k c) -> p f k c", f=8, k=KC)
            for f in range(8):
                fc = l * 48 + g * 8 + f
                for kc in range(KC):
                    sc.op("pe", lambda e, bv=bv, f=f, kc=kc, fc=fc: e.matmul(
                        psb[0][:, fc * nseq:(fc + 1) * nseq], lhsT=bv[:, f, kc, :], rhs=cactv[:, kc, :],
                        start=(kc == 0), stop=(kc == KC - 1)),
                        r=[f"adaw{bi}", "cact"], w=[PS[0]])
    psm = psb[0][:, 0:96 * nseq].rearrange("p (f s) -> p f s", s=nseq)
    for s in range(nseq):
        sc.op("dve", lambda e, s=s: e.tensor_tensor(out=modv[s][:], in0=psm[:, :, s], in1=adab[:], op=ALU.add),
              r=[PS[0], "adab"], w=[f"modv{s}"])
        for l in range(2):
            sc.op("dve", lambda e, s=s, l=l: e.scalar_tensor_tensor(
                out=a1[s][l][:], in0=modslice(s, l, 1), scalar=1.0, in1=n1g[:, 8 * l:8 * l + 8],
                op0=ALU.add, op1=ALU.mult), r=[f"modv{s}", "n1g"], w=[f"a1_{s}_{l}"])
            sc.op("dve", lambda e, s=s, l=l: e.scalar_tensor_tensor(
                out=a2[s][l][:], in0=modslice(s, l, 4), scalar=1.0, in1=n2g[:, 8 * l:8 * l + 8],
                op0=ALU.add, op1=ALU.mult), r=[f"modv{s}", "n2g"], w=[f"a2_{s}_{l}"])
    sc.barrier()
    ar.reset(m0)

    rr = {"ps": 0}

    def norm_mod(s, a_t, b_ap, akey, final=False):
        m = ar.mark()
        sq = A("sq", KC * 512, BF16)
        sqv = v3(sq, 512)
        lnv = A("lnv", 512, F32)
        R = [A("R0", 512, F32), A("R1", 512, F32)]
        tmp = [A(f"ntmp{i}", 512, F32) for i in range(4)]
        for tb in range(NTB):
            ts = slice(tb * 512, tb * 512 + 512)
            for kc in range(KC):
                if kc % 2 == 0:
                    sc.op("act", lambda e, kc=kc, ts=ts: e.activation(out=sqv[:, kc, :], in_=xTv[:, kc, ts], func=AF.Square),
                          r=[("xT", kc, tb)], w=[("sq", kc)])
                else:
                    sc.op("pool", lambda e, kc=kc, ts=ts: e.tensor_tensor(out=sqv[:, kc, :], in0=xTv[:, kc, ts], in1=xTv[:, kc, ts], op=ALU.mult),
                          r=[("xT", kc, tb)], w=[("sq", kc)])
            pb = 7
            for kc in range(KC):
                sc.op("pe", lambda e, kc=kc: e.matmul(psb[pb][:], lhsT=ones_b[:], rhs=sqv[:, kc, :], start=(kc == 0), stop=(kc == KC - 1)),
                      r=[("sq", kc), "ones_b"], w=[PS[pb]])
            Rt = R[tb % 2]
            Rk = f"R{tb % 2}"
            sc.op("act", lambda e: e.activation(out=lnv[:], in_=psb[pb][:], func=AF.Ln, bias=eps_t[:], scale=1.0 / D),
                  r=[PS[pb], "eps_t"], w=["lnv"])
            sc.op("act", lambda e, Rt=Rt: e.activation(out=Rt[:], in_=lnv[:], func=AF.Exp, scale=-0.5), r=["lnv"], w=[Rk])
            for kc in range(KC):
                ti = kc % 4
                tt = tmp[ti]
                if final:
                    sc.op("dve", lambda e, kc=kc, ts=ts, tt=tt, Rt=Rt: e.scalar_tensor_tensor(
                        out=tt[:], in0=xTv[:, kc, ts], scalar=fg[:, kc:kc + 1], in1=Rt[:], op0=ALU.mult, op1=ALU.mult),
                        r=[("xT", kc, tb), Rk, "fg"], w=[("ntmp", ti)])
                    sc.op("sp", lambda e, kc=kc, ts=ts, tt=tt: e.dma_start(out=d_out[s, :, kc, ts], in_=tt[:]),
                          r=[("ntmp", ti)], w=[("out", s, kc, tb)], dma=True)
                else:
                    sc.op("dve", lambda e, kc=kc, ts=ts, tt=tt, Rt=Rt: e.tensor_tensor(out=tt[:], in0=xTv[:, kc, ts], in1=Rt[:], op=ALU.mult),
                          r=[("xT", kc, tb), Rk], w=[("ntmp", ti)])
                    if kc % 2 == 0:
                        sc.op("act", lambda e, kc=kc, ts=ts, tt=tt: e.activation(
                            out=hTv[:, kc, ts], in_=tt[:], func=AF.Identity, bias=b_ap[:, kc:kc + 1], scale=a_t[:, kc:kc + 1]),
                            r=[("ntmp", ti), akey], w=[("hT", kc, tb)])
                    else:
                        sc.op("pool", lambda e, kc=kc, ts=ts, tt=tt: e.tensor_scalar(
                            out=hTv[:, kc, ts], in0=tt[:], scalar1=a_t[:, kc:kc + 1], scalar2=b_ap[:, kc:kc + 1],
                            op0=ALU.mult, op1=ALU.add),
                            r=[("ntmp", ti), akey], w=[("hT", kc, tb)])
        sc.barrier()
        ar.reset(m)

    def resid_add(pbank, gate_ap, oc, tb):
        ts = slice(tb * 512, tb * 512 + 512)
        sc.op("dve", lambda e: e.scalar_tensor_tensor(out=xTv[:, oc, ts], in0=psb[pbank][:], scalar=gate_ap[:, oc:oc + 1],
                                                      in1=xTv[:, oc, ts], op0=ALU.mult, op1=ALU.add),
              r=[PS[pbank], ("xT", oc, tb)], w=[("xT", oc, tb)])

    def ffn(s, l):
        m = ar.mark()
        GC = 6
        groups = [list(range(i, min(i + GC, NCH))) for i in range(0, NCH, GC)]
        win = [A(f"win{i}", KC * 256, BF16) for i in range(3)]
        wout = [A(f"wout{i}", D, BF16) for i in range(2 * GC)]
        hid = [A(f"hid{i}", S, BF16) for i in range(2 * GC)]
        sg = [A(f"sg{i}", 512, F32) for i in range(2)]
        gate = modslice(s, l, 5)
        nwin = 0
        nsg = 0
        for gi, grp in enumerate(groups):
            par = gi % 2
            for ci, c in enumerate(grp):
                wi = nwin % 3
                nwin += 1
                wt = win[wi]
                wv = v3(wt, 256)
                ld(wt[:], d_ffn_in[l, c, :, :], [f"win{wi}"], eng="pool")
                ho = par * GC + ci
                ld(wout[ho][:], d_ffn_out[l, c, :, :], [f"wout{ho}"], eng="pool")
                for tb in range(NTB):
                    ts = slice(tb * 512, tb * 512 + 512)
                    pg = rr["ps"] % 3
                    pu = 3 + rr["ps"] % 3
                    rr["ps"] += 1
                    for kc in range(KC):
                        sc.op("pe", lambda e, kc=kc, ts=ts, wv=wv, pg=pg: e.matmul(psb[pg][:], lhsT=wv[:, kc, 0:128], rhs=hTv[:, kc, ts],
                                                                                 start=(kc == 0), stop=(kc == KC - 1)),
                              r=[f"win{wi}", ("hT", kc, tb)], w=[PS[pg]])
                    for kc in range(KC):
                        sc.op("pe", lambda e, kc=kc, ts=ts, wv=wv, pu=pu: e.matmul(psb[pu][:], lhsT=wv[:, kc, 128:256], rhs=hTv[:, kc, ts],
                                                                                 start=(kc == 0), stop=(kc == KC - 1)),
                              r=[f"win{wi}", ("hT", kc, tb)], w=[PS[pu]])
                    si = nsg % 2
                    nsg += 1
                    sc.op("act", lambda e, pg=pg, si=si: e.activation(out=sg[si][:], in_=psb[pg][:], func=AF.Silu),
                          r=[PS[pg]], w=[f"sg{si}"])
                    sc.op("dve", lambda e, pu=pu, si=si, ho=ho, ts=ts: e.tensor_tensor(out=hid[ho][:, ts], in0=psb[pu][:], in1=sg[si][:], op=ALU.mult),
                          r=[PS[pu], f"sg{si}"], w=[("hid", ho, tb)])
            for tb in range(NTB):
                ts = slice(tb * 512, tb * 512 + 512)
                for oc in range(KC):
                    py = 6 + (oc % 2)
                    for ci, c in enumerate(grp):
                        ho = par * GC + ci
                        sc.op("pe", lambda e, ho=ho, oc=oc, ts=ts, py=py, ci=ci: e.matmul(
                            psb[py][:], lhsT=wout[ho][:, oc * 128:(oc + 1) * 128], rhs=hid[ho][:, ts],
                            start=(ci == 0), stop=(ci == len(grp) - 1)),
                            r=[f"wout{ho}", ("hid", ho, tb)], w=[PS[py]])
                    resid_add(py, gate, oc, tb)
        sc.barrier()
        ar.reset(m)

    def fox(s, l):
        m = ar.mark()
        gate = modslice(s, l, 2)
        wf = A("wf", KC * 16, BF16)
        wfv = v3(wf, 16)
        lsp = A("lsp", S, F32)
        negcum = A("negcum", S, F32)
        cumb = A("cumb", S, BF16)
        kbias = A("kbias", NTT * 16, F32)
        kbv = v3(kbias, 16)
        qk = [[[A(f"qk{b}_{h}_{i}", S, BF16) for i in range(2)] for h in range(2)] for b in range(2)]
        vaug = [A(f"vaug{b}", NTT * 192, BF16) for b in range(2)]
        oTp = [A(f"oTp{b}", S, BF16) for b in range(2)]
        wg = [A(f"wg{b}", KC * 384, BF16) for b in range(2)]
        wo = [A(f"wo{b}", D, BF16) for b in range(2)]
        pT = [A(f"pT{i}", 512, BF16) for i in range(4)]
        rc = [A(f"rc{i}", 512, F32) for i in range(2)]
        for b in range(2):
            vv = v3(vaug[b], 192)
            sc.op("pool", lambda e, vv=vv: e.memset(vv[:, :, 64:128], 1.0), w=[f"vaug{b}"])
            for h in range(2):
                sc.op("pool", lambda e, b=b, h=h: e.memset(qk[b][h][1][64:96, :], 1.0), w=[("k", b, h)])
                sc.op("pool", lambda e, b=b, h=h: e.memset(qk[b][h][0][64:96, :], 0.0), w=[("q", b, h)])
        ld(wf[:], d_fox_f[:, :], ["wf"], eng="pool")
        for tb in range(NTB):
            ts = slice(tb * 512, tb * 512 + 512)
            pb = 7
            for kc in range(KC):
                sc.op("pe", lambda e, kc=kc, ts=ts: e.matmul(psb[pb][0:16, :], lhsT=wfv[:, kc, :], rhs=hTv[:, kc, ts],
                                                           start=(kc == 0), stop=(kc == KC - 1)),
                      r=["wf", ("hT", kc, tb)], w=[PS[pb]])
            sc.op("act", lambda e, ts=ts: e.activation(out=lsp[0:16, ts], in_=psb[pb][0:16, :], func=AF.Exp, bias=fox_nbf[0:16, :], scale=-1.0),
                  r=[PS[pb], "fox_nbf"], w=["lsp"])
        sc.op("act", lambda e: e.activation(out=lsp[0:16, :], in_=lsp[0:16, :], func=AF.Ln, bias=one_t[0:16, :], scale=1.0),
              r=["lsp", "one_t"], w=["lsp"])
        sc.op("dve", lambda e: e.tensor_tensor_scan(out=negcum[0:16, :], data0=lsp[0:16, :], data1=lsp[0:16, :], initial=0.0,
                                                    op0=ALU.add, op1=ALU.max), r=["lsp"], w=["negcum"])
        sc.op("dve", lambda e: e.tensor_scalar_mul(out=cumb[0:16, :], in0=negcum[0:16, :], scalar1=-1.0), r=["negcum"], w=["cumb"])
        for j in range(NTT):
            sc.op("pe", lambda e, j=j: e.transpose(out=psb[6][:, 16 * j:16 * j + 16], in_=negcum[0:16, 128 * j:128 * j + 128],
                                                   identity=ident_f[0:16, 0:16]),
                  r=["negcum", "ident_f"], w=[PS[6]])
        sc.op("dve", lambda e: e.tensor_copy(out=kbias[:], in_=psb[6][:, 0:NTT * 16]), r=[PS[6]], w=["kbias"])
        dump(0, negcum[0:16, 0:512], ["negcum"], np_=16)
        dump(1, kbias[:, 0:NTT * 16], ["kbias"], ncol=NTT * 16)

        nexp = 0
        nacc = 0
        for g in range(8):
            b = g % 2
            wgt = wg[b]
            wgv = v3(wgt, 384)
            ld(wgt[:], d_fox_in[g, :, :], [f"wg{b}"], eng="pool")
            ld(wo[b][:], d_fox_out[g, :, :], [f"wo{b}"], eng="pool")
            for h in range(2):
                hh = 2 * g + h
                sc.op("sp", lambda e, b=b, h=h, hh=hh: e.dma_start(out=qk[b][h][0][64:65, :], in_=cumb[hh:hh + 1, :]),
                      r=["cumb"], w=[("q", b, h)], dma=True)
            for qi in range(2):
                for tb in range(NTB):
                    ts = slice(tb * 512, tb * 512 + 512)
                    pb = rr["ps"] % 2
                    rr["ps"] += 1
                    for kc in range(KC):
                        sc.op("pe", lambda e, kc=kc, ts=ts, pb=pb, qi=qi, wgv=wgv: e.matmul(
                            psb[pb][:], lhsT=wgv[:, kc, 128 * qi:128 * qi + 128], rhs=hTv[:, kc, ts],
                            start=(kc == 0), stop=(kc == KC - 1)),
                            r=[f"wg{b}", ("hT", kc, tb)], w=[PS[pb]])
                    key = "q" if qi == 0 else "k"
                    scl = 0.125 if qi == 0 else 1.0
                    sc.op("dve", lambda e, pb=pb, ts=ts, qi=qi, scl=scl: e.tensor_scalar_mul(
                        out=qk[b][0][qi][0:64, ts], in0=psb[pb][0:64, :], scalar1=scl),
                        r=[PS[pb]], w=[(key, b, 0)])
                    sc.op("pool" if False else "dve", lambda e, pb=pb, ts=ts, qi=qi, scl=scl: e.tensor_scalar_mul(
                        out=qk[b][1][qi][0:64, ts], in0=psb[pb][64:128, :], scalar1=scl),
                        r=[PS[pb]], w=[(key, b, 1)])
            vv = v3(vaug[b], 192)
            for j4 in range(NTT // 4):
                pb = rr["ps"] % 2
                rr["ps"] += 1
                for jj in range(4):
                    j = 4 * j4 + jj
                    tb = j // 4
                    for kc in range(KC):
                        sc.op("pe", lambda e, kc=kc, j=j, jj=jj, pb=pb, wgv=wgv: e.matmul(
                            psb[pb][:, 128 * jj:128 * jj + 128], lhsT=hTv[:, kc, 128 * j:128 * j + 128], rhs=wgv[:, kc, 256:384],
                            start=(kc == 0), stop=(kc == KC - 1)),
                            r=[f"wg{b}", ("hT", kc, tb)], w=[PS[pb]])
                pv = psb[pb][:].rearrange("p (a c) -> p a c", c=128)
                sc.op("dve", lambda e, pv=pv, vv=vv, j4=j4: e.tensor_copy(out=vv[:, 4 * j4:4 * j4 + 4, 0:64], in_=pv[:, :, 0:64]),
                      r=[PS[pb]], w=[f"vaug{b}"])
                sc.op("dve", lambda e, pv=pv, vv=vv, j4=j4: e.tensor_copy(out=vv[:, 4 * j4:4 * j4 + 4, 128:192], in_=pv[:, :, 64:128]),
                      r=[PS[pb]], w=[f"vaug{b}"])
            dump(2, qk[b][0][0][0:96, 0:512], [("q", b, 0)], np_=96)
            dump(3, qk[b][0][1][0:96, 0:512], [("k", b, 0)], np_=96)
            dump(4, vaug[b][:, 0:512], [f"vaug{b}"])
            for h in range(2):
                hh = 2 * g + h
                qt = qk[b][h][0]
                kt = qk[b][h][1]
                for qb in range(NTB):
                    q0 = qb * 512
                    pa = 5 + nacc % 2
                    nacc += 1
                    nj = 4 * qb + 4
                    for j in range(nj):
                        mdiag = j - 4 * qb
                        c0 = 128 * mdiag if mdiag > 0 else 0
                        pss = 2 + nexp % 3
                        pti = nexp % 4
                        nexp += 1
                        if mdiag >= 0:
                            sc.op("pe", lambda e, j=j, c0=c0, pss=pss, qt=qt, kt=kt, q0=q0: e.matmul(
                                psb[pss][:, c0:c0 + 128], lhsT=kt[0:96, 128 * j:128 * j + 128], rhs=qt[0:96, q0 + c0:q0 + c0 + 128],
                                start=True, stop=False),
                                r=[("k", b, h), ("q", b, h)], w=[PS[pss]])
                            sc.op("pe", lambda e, c0=c0, pss=pss: e.matmul(
                                psb[pss][:, c0:c0 + 128], lhsT=ident_b[:], rhs=negmask[:], start=False, stop=True),
                                r=["ident_b", "negmask"], w=[PS[pss]])
                            if c0 + 128 < 512:
                                sc.op("pe", lambda e, j=j, c0=c0, pss=pss, qt=qt, kt=kt, q0=q0: e.matmul(
                                    psb[pss][:, c0 + 128:512], lhsT=kt[0:96, 128 * j:128 * j + 128], rhs=qt[0:96, q0 + c0 + 128:q0 + 512],
                                    start=True, stop=True),
                                    r=[("k", b, h), ("q", b, h)], w=[PS[pss]])
                        else:
                            sc.op("pe", lambda e, j=j, pss=pss, qt=qt, kt=kt, q0=q0: e.matmul(
                                psb[pss][:], lhsT=kt[0:96, 128 * j:128 * j + 128], rhs=qt[0:96, q0:q0 + 512],
                                start=True, stop=True),
                                r=[("k", b, h), ("q", b, h)], w=[PS[pss]])
                        sc.op("act", lambda e, j=j, c0=c0, pss=pss, pti=pti, hh=hh: e.activation(
                            out=pT[pti][:, c0:512], in_=psb[pss][:, c0:512], func=AF.Exp, bias=kbv[:, j, hh:hh + 1], scale=1.0),
                            r=[PS[pss], "kbias"], w=[f"pT{pti}"])
                        dump(5, pT[pti][:, 0:512], [f"pT{pti}"])
                        voff = 0 if h == 0 else 64
                        sc.op("pe", lambda e, j=j, c0=c0, pa=pa, pti=pti, vv=vv, voff=voff, nj=nj: e.matmul(
                            psb[pa][:, c0:512], lhsT=vv[:, j, voff:voff + 128], rhs=pT[pti][:, c0:512],
                            start=(j == 0), stop=(j == nj - 1)),
                            r=[f"vaug{b}", f"pT{pti}"], w=[PS[pa]])
                    ri = nacc % 2
                    ts = slice(q0, q0 + 512)
                    if h == 0:
                        sc.op("dve", lambda e, pa=pa, ri=ri: e.reciprocal(out=rc[ri][0:64, :], in_=psb[pa][64:128, :]),
                              r=[PS[pa]], w=[f"rc{ri}"])
                        sc.op("dve", lambda e, pa=pa, ri=ri, ts=ts: e.tensor_tensor(out=oTp[b][0:64, ts], in0=psb[pa][0:64, :], in1=rc[ri][0:64, :], op=ALU.mult),
                              r=[PS[pa], f"rc{ri}"], w=[("oTp", b, qb)])
                    else:
                        sc.op("dve", lambda e, pa=pa, ri=ri: e.reciprocal(out=rc[ri][64:128, :], in_=psb[pa][0:64, :]),
                              r=[PS[pa]], w=[f"rc{ri}"])
                        sc.op("dve", lambda e, pa=pa, ri=ri, ts=ts: e.tensor_tensor(out=oTp[b][64:128, ts], in0=psb[pa][64:128, :], in1=rc[ri][64:128, :], op=ALU.mult),
                              r=[PS[pa], f"rc{ri}"], w=[("oTp", b, qb)])
            dump(6, rc[0][:, 0:512], ["rc0"])
            dump(7, oTp[b][:, 0:512], [("oTp", b, 0)])
            for tb in range(NTB):
                ts = slice(tb * 512, tb * 512 + 512)
                for oc in range(KC):
                    py = rr["ps"] % 2
                    rr["ps"] += 1
                    sc.op("pe", lambda e, oc=oc, ts=ts, py=py: e.matmul(psb[py][:], lhsT=wo[b][:, oc * 128:(oc + 1) * 128], rhs=oTp[b][:, ts],
                                                                      start=True, stop=True),
                          r=[f"wo{b}", ("oTp", b, tb)], w=[PS[py]])
                    resid_add(py, gate, oc, tb)
        sc.barrier()
        ar.reset(m)

    def gla(s, l):
        m = ar.mark()
        gate = modslice(s, l, 2)
        NCK = S // 64
        wa = A("wa", KC * 16, BF16)
        wav = v3(wa, 16)
        alrT = A("alrT", S, BF16)
        wh = A("wh", KC * 768, BF16)
        whv = v3(wh, 768)
        woh = A("woh", 2 * D, BF16)
        wohv = v3(woh, D)
        qin = A("qin", S, BF16)
        kin = A("kin", S, BF16)
        kdec = A("kdec", NTT * 128, BF16)
        kdv = v3(kdec, 128)
        vtm = A("vtm", NTT * 256, BF16)
        vtv = v3(vtm, 256)
        rs = A("rs", 2 * S, BF16)
        rsv = v3(rs, S)
        oT2 = A("oT2", 2 * S, BF16)
        oT2v = v3(oT2, S)
        ebl = A("ebl", NCK, F32)
        sp_t = [A(f"sp{i}", 512, F32) for i in range(2)]
        Eq = A("Eq", 512, F32)
        Ek = A("Ek", 512, F32)
        Edec = A("Edec", 512, F32)
        AT = [A(f"AT{i}", 512, BF16) for i in range(2)]
        Sf = [A(f"Sf{i}", 256, F32) for i in range(2)]
        Sb = [A(f"Sb{i}", 256, BF16) for i in range(2)]
        sqo = A("sqo", 1024, BF16)
        sqov = v3(sqo, 512)
        lno = A("lno", 512, F32)
        Ro = A("Ro", 512, F32)
        t1 = [A(f"t1_{i}", 512, F32) for i in range(2)]
        eps2 = eps_t

        ld(wa[:], d_gla_a[:, :], ["wa"], eng="pool")
        sc.op("pool", lambda e: e.memset(alrT[0:32, :], 1.0), w=["alrT"])
        for tb in range(NTB):
            ts = slice(tb * 512, tb * 512 + 512)
            for kc in range(KC):
                sc.op("pe", lambda e, kc=kc, ts=ts: e.matmul(psb[7][0:16, :], lhsT=wav[:, kc, :], rhs=hTv[:, kc, ts],
                                                           start=(kc == 0), stop=(kc == KC - 1)),
                      r=["wa", ("hT", kc, tb)], w=[PS[7]])
            sc.op("dve", lambda e, ts=ts: e.tensor_copy(out=alrT[0:16, ts], in_=psb[7][0:16, :]), r=[PS[7]], w=["alrT"])

        for h in range(4):
            ld(wh[:], d_gla_in[h, :, :], ["wh"], eng="pool")
            ld(woh[:], d_gla_out[h, :, :], ["woh"], eng="pool")
            for tb in range(NTB):
                ts = slice(tb * 512, tb * 512 + 512)
                spt = sp_t[tb % 2]
                spk = f"sp{tb % 2}"
                spv = v3(spt, 128)
                for jj in range(4):
                    j = 4 * tb + jj
                    sc.op("pe", lambda e, j=j, jj=jj: e.matmul(psb[0][:, 128 * jj:128 * jj + 128], lhsT=alrT[0:17, 128 * j:128 * j + 128],
                                                               rhs=gla_a2[0:17, 128 * h:128 * h + 128], start=True, stop=True),
                          r=["alrT", "gla_a2"], w=[PS[0]])
                sc.op("act", lambda e, spt=spt: e.activation(out=spt[:], in_=psb[0][:], func=AF.Exp, scale=-1.0), r=[PS[0]], w=[spk])
                sc.op("act", lambda e, spt=spt: e.activation(out=spt[:], in_=spt[:], func=AF.Ln, bias=one_t[:], scale=1.0),
                      r=[spk, "one_t"], w=[spk])
                for jj in range(4):
                    sc.op("pe", lambda e, jj=jj, spv=spv: e.matmul(psb[1][:, 128 * jj:128 * jj + 128], lhsT=spv[:, jj, :], rhs=tri_incl[:],
                                                                  start=True, stop=True),
                          r=[spk, "tri_incl"], w=[PS[1]])
                sc.op("act", lambda e: e.activation(out=Eq[:], in_=psb[1][:], func=AF.Exp, scale=-1.0 / 16.0), r=[PS[1]], w=["Eq"])
                sc.op("act", lambda e: e.activation(out=Ek[:], in_=psb[1][:], func=AF.Exp, scale=1.0 / 16.0), r=[PS[1]], w=["Ek"])
                for kc in range(KC):
                    sc.op("pe", lambda e, kc=kc, ts=ts: e.matmul(psb[2][:], lhsT=whv[:, kc, 0:128], rhs=hTv[:, kc, ts],
                                                               start=(kc == 0), stop=(kc == KC - 1)),
                          r=["wh", ("hT", kc, tb)], w=[PS[2]])
                sc.op("dve", lambda e, ts=ts: e.scalar_tensor_tensor(out=qin[:, ts], in0=psb[2][:], scalar=float(128 ** -0.5), in1=Eq[:],
                                                                      op0=ALU.mult, op1=ALU.mult),
                      r=[PS[2], "Eq"], w=[("qin", tb)])
                for kc in range(KC):
                    sc.op("pe", lambda e, kc=kc, ts=ts: e.matmul(psb[3][:], lhsT=whv[:, kc, 128:256], rhs=hTv[:, kc, ts],
                                                               start=(kc == 0), stop=(kc == KC - 1)),
                          r=["wh", ("hT", kc, tb)], w=[PS[3]])
                sc.op("dve", lambda e, ts=ts: e.tensor_tensor(out=kin[:, ts], in0=psb[3][:], in1=Ek[:], op=ALU.mult),
                      r=[PS[3], "Ek"], w=[("kin", tb)])
                Eqv = Eq[:].rearrange("p (c t) -> p c t", t=64)
                sc.op("dve", lambda e, tb=tb, Eqv=Eqv: e.tensor_copy(out=ebl[:, 8 * tb:8 * tb + 8], in_=Eqv[:, :, 63]),
                      r=["Eq"], w=["ebl"])
                for jj in range(4):
                    sc.op("pe", lambda e, jj=jj, spv=spv: e.matmul(psb[4][:, 128 * jj:128 * jj + 128], lhsT=tri_after[:], rhs=spv[:, jj, :],
                                                                  start=True, stop=True),
                          r=[spk, "tri_after"], w=[PS[4]])
                sc.op("act", lambda e: e.activation(out=Edec[:], in_=psb[4][:], func=AF.Exp, scale=-1.0 / 16.0), r=[PS[4]], w=["Edec"])
                for jj in range(4):
                    j = 4 * tb + jj
                    for kc in range(KC):
                        sc.op("pe", lambda e, kc=kc, j=j, jj=jj: e.matmul(psb[5][:, 128 * jj:128 * jj + 128], lhsT=hTv[:, kc, 128 * j:128 * j + 128],
                                                                         rhs=whv[:, kc, 128:256], start=(kc == 0), stop=(kc == KC - 1)),
                              r=["wh", ("hT", kc, tb)], w=[PS[5]])
                sc.op("dve", lambda e, tb=tb: e.tensor_tensor(out=kdec[:, 512 * tb:512 * tb + 512], in0=psb[5][:], in1=Edec[:], op=ALU.mult),
                      r=[PS[5], "Edec"], w=[("kdec", tb)])
                for j2 in range(2):
                    pb = 6 + j2
                    for jj in range(2):
                        j = 4 * tb + 2 * j2 + jj
                        for kc in range(KC):
                            sc.op("pe", lambda e, kc=kc, j=j, jj=jj, pb=pb: e.matmul(psb[pb][:, 256 * jj:256 * jj + 256], lhsT=hTv[:, kc, 128 * j:128 * j + 128],
                                                                                   rhs=whv[:, kc, 256:512], start=(kc == 0), stop=(kc == KC - 1)),
                                  r=["wh", ("hT", kc, tb)], w=[PS[pb]])
                    j0 = 4 * tb + 2 * j2
                    sc.op("dve", lambda e, pb=pb, j0=j0: e.tensor_copy(out=vtm[:, 256 * j0:256 * j0 + 512], in_=psb[pb][:]),
                          r=[PS[pb]], w=[("vtm", tb)])
            for c in range(2):
                for tb in range(NTB):
                    ts = slice(tb * 512, tb * 512 + 512)
                    pb = 2 + (rr["ps"] % 2)
                    rr["ps"] += 1
                    for kc in range(KC):
                        sc.op("pe", lambda e, kc=kc, ts=ts, pb=pb, c=c: e.matmul(psb[pb][:], lhsT=whv[:, kc, 512 + 128 * c:640 + 128 * c], rhs=hTv[:, kc, ts],
                                                                               start=(kc == 0), stop=(kc == KC - 1)),
                              r=["wh", ("hT", kc, tb)], w=[PS[pb]])
                    sc.op("act", lambda e, pb=pb, c=c, ts=ts: e.activation(out=rsv[:, c, ts], in_=psb[pb][:], func=AF.Silu),
                          r=[PS[pb]], w=[("rs", c, tb)])
            sc.op("pool", lambda e: e.memset(Sf[0][:], 0.0), w=["Sf0"])
            sc.op("pool", lambda e: e.memset(Sb[0][:], 0.0), w=["Sb0"])
            cur = 0
            for tb in range(NTB):
                ts = slice(tb * 512, tb * 512 + 512)
                at = AT[tb % 2]
                atk = f"AT{tb % 2}"
                for jj in range(4):
                    t0 = 512 * tb + 128 * jj
                    sc.op("pe", lambda e, jj=jj, t0=t0: e.matmul(psb[0][:, 128 * jj:128 * jj + 128], lhsT=kin[:, t0:t0 + 128], rhs=qin[:, t0:t0 + 128],
                                                               start=True, stop=True),
                          r=[("kin", tb), ("qin", tb)], w=[PS[0]])
                sc.op("dve", lambda e, at=at: e.tensor_tensor(out=at[:], in0=psb[0][:], in1=tri4[:], op=ALU.mult),
                      r=[PS[0], "tri4"], w=[atk])
                po = [1, 2] if tb % 2 == 0 else [3, 4]
                for cn in range(8):
                    n = 8 * tb + cn
                    j = n // 2
                    hf = 64 * (n % 2)
                    tcol = slice(64 * cn, 64 * cn + 64)
                    t0 = 64 * n
                    for c in range(2):
                        sc.op("pe", lambda e, c=c, t0=t0, tcol=tcol, cur=cur, po=po: e.matmul(
                            psb[po[c]][:, tcol], lhsT=Sb[cur][:, 128 * c:128 * c + 128], rhs=qin[:, t0:t0 + 64], start=True, stop=False),
                            r=[f"Sb{cur}", ("qin", tb)], w=[PS[po[c]]])
                        sc.op("pe", lambda e, c=c, j=j, hf=hf, tcol=tcol, at=at, po=po: e.matmul(
                            psb[po[c]][:, tcol], lhsT=vtv[hf:hf + 64, j, 128 * c:128 * c + 128], rhs=at[hf:hf + 64, tcol], start=False, stop=True),
                            r=[("vtm", tb), atk], w=[PS[po[c]]])
                    pk = 5 + (n % 2)
                    sc.op("pe", lambda e, j=j, hf=hf, pk=pk: e.matmul(psb[pk][:, 0:256], lhsT=kdv[hf:hf + 64, j, :], rhs=vtv[hf:hf + 64, j, :],
                                                                     start=True, stop=True),
                          r=[("kdec", tb), ("vtm", tb)], w=[PS[pk]])
                    nxt = 1 - cur
                    sc.op("dve", lambda e, n=n, pk=pk, cur=cur, nxt=nxt: e.scalar_tensor_tensor(
                        out=Sf[nxt][:], in0=Sf[cur][:], scalar=ebl[:, n:n + 1], in1=psb[pk][:, 0:256], op0=ALU.mult, op1=ALU.add),
                        r=[f"Sf{cur}", "ebl", PS[pk]], w=[f"Sf{nxt}"])
                    sc.op("pool", lambda e, nxt=nxt: e.tensor_copy(out=Sb[nxt][:], in_=Sf[nxt][:]), r=[f"Sf{nxt}"], w=[f"Sb{nxt}"])
                    cur = nxt
                for c in range(2):
                    sc.op("act", lambda e, c=c, po=po: e.activation(out=sqov[:, c, :], in_=psb[po[c]][:], func=AF.Square),
                          r=[PS[po[c]]], w=[("sqo", c)])
                for c in range(2):
                    sc.op("pe", lambda e, c=c: e.matmul(psb[7][:], lhsT=ones_b[:], rhs=sqov[:, c, :], start=(c == 0), stop=(c == 1)),
                          r=[("sqo", c), "ones_b"], w=[PS[7]])
                sc.op("act", lambda e: e.activation(out=lno[:], in_=psb[7][:], func=AF.Ln, bias=eps2[:], scale=1.0 / 256.0),
                      r=[PS[7], "eps_t"], w=["lno"])
                sc.op("act", lambda e: e.activation(out=Ro[:], in_=lno[:], func=AF.Exp, scale=-0.5), r=["lno"], w=["Ro"])
                for c in range(2):
                    gi = 2 * h + c
                    sc.op("dve", lambda e, c=c, gi=gi, po=po: e.scalar_tensor_tensor(
                        out=t1[c][:], in0=psb[po[c]][:], scalar=gla_go[:, gi:gi + 1], in1=Ro[:], op0=ALU.mult, op1=ALU.mult),
                        r=[PS[po[c]], "Ro", "gla_go"], w=[f"t1_{c}"])
                    sc.op("pool", lambda e, c=c, ts=ts: e.tensor_tensor(out=oT2v[:, c, ts], in0=t1[c][:], in1=rsv[:, c, ts], op=ALU.mult),
                          r=[f"t1_{c}", ("rs", c, tb)], w=[("oT2", c, tb)])
            for tb in range(NTB):
                ts = slice(tb * 512, tb * 512 + 512)
                for oc in range(KC):
                    py = 5 + (rr["ps"] % 2)
                    rr["ps"] += 1
                    for c in range(2):
                        sc.op("pe", lambda e, c=c, oc=oc, ts=ts, py=py: e.matmul(psb[py][:], lhsT=wohv[:, c, oc * 128:(oc + 1) * 128], rhs=oT2v[:, c, ts],
                                                                               start=(c == 0), stop=(c == 1)),
                              r=["woh", ("oT2", c, tb)], w=[PS[py]])
                    resid_add(py, gate, oc, tb)
        sc.barrier()
        ar.reset(m)

    for s in range(nseq):
        for kc in range(KC):
            for tb in range(NTB):
                ts = slice(tb * 512, tb * 512 + 512)
                sc.op("sp", lambda e, kc=kc, ts=ts: e.dma_start(out=xTv[:, kc, ts], in_=d_xT[s, :, kc, ts]),
                      w=[("xT", kc, tb)], dma=True)
        the_plan = plan if plan is not None else [(k, l) for l in range(depth) for k in ("mix", "ffn")]
        for kind, l in the_plan:
            if kind == "mix":
                norm_mod(s, a1[s][l], modslice(s, l, 0), f"a1_{s}_{l}")
                if l % 2 == 0:
                    fox(s, l)
                else:
                    gla(s, l)
            else:
                norm_mod(s, a2[s][l], modslice(s, l, 3), f"a2_{s}_{l}")
                ffn(s, l)
        norm_mod(s, None, None, None, final=True)

    sc.finalize()
    sem_names = [("pe",), ("act",), ("dve",), ("pool",)]
    sems = {}
    import contextlib
    with contextlib.ExitStack() as es:
        for e in ["pe", "act", "dve", "pool"]:
            sems[e] = es.enter_context(nc.semaphore(f"s_{e}"))
        for q in ["sp", "pool"]:
            for i in range(sc.ring):
                sems[(q, i)] = es.enter_context(nc.semaphore(f"d_{q}{i}"))
        block = es.enter_context(nc.Block())

        @block.sync
        def _(e):
            sc.emit_engine("sp", e, sems)

        @block.tensor
        def _(e):
            sc.emit_engine("pe", e, sems)

        @block.scalar
        def _(e):
            sc.emit_engine("act", e, sems)

        @block.vector
        def _(e):
            sc.emit_engine("dve", e, sems)

        @block.gpsimd
        def _(e):
            sc.emit_engine("pool", e, sems)
    return nc


def _kmajor(w):
    C = w.shape[1]
    return np.ascontiguousarray(w.reshape(KC, 128, C).transpose(1, 0, 2))


def prepare_shared(inp):
    f = np.float32
    sh = {}
    ada_w = np.asarray(inp["ada_w"], f)
    arr = ada_w.reshape(2, KC, 128, 6, 8, 128).transpose(0, 3, 2, 4, 1, 5)
    sh["adaw"] = np.ascontiguousarray(arr).reshape(2, 6, 128, 8 * KC * 128)
    ada_b = np.asarray(inp["ada_b"], f)
    sh["adab"] = np.ascontiguousarray(ada_b.reshape(2, 48, 128).transpose(2, 0, 1)).reshape(128, 96)
    sh["n1g"] = np.ascontiguousarray(np.asarray(inp["norm1_g"], f).reshape(2, KC, 128).transpose(2, 0, 1)).reshape(128, 16)
    sh["n2g"] = np.ascontiguousarray(np.asarray(inp["norm2_g"], f).reshape(2, KC, 128).transpose(2, 0, 1)).reshape(128, 16)
    sh["fg"] = np.ascontiguousarray(np.asarray(inp["final_g"], f).reshape(KC, 128).T)
    wi = np.asarray(inp["ffn_w_in"], f)
    ffn_in = np.empty((2, NCH, 128, KC, 256), f)
    for l in range(2):
        km = _kmajor(wi[l])
        for c in range(NCH):
            ffn_in[l, c, :, :, 0:128] = km[:, :, c * 128:(c + 1) * 128]
            ffn_in[l, c, :, :, 128:256] = km[:, :, FFN_H + c * 128:FFN_H + (c + 1) * 128]
    sh["ffn_in"] = ffn_in.reshape(2, NCH, 128, KC * 256)
    sh["ffn_out"] = np.ascontiguousarray(np.asarray(inp["ffn_w_out"], f).reshape(2, NCH, 128, D))
    fw = _kmajor(np.asarray(inp["fox_w_in"], f)[0])
    fox_in = np.empty((8, 128, KC, 384), f)
    for g in range(8):
        fox_in[g, :, :, 0:128] = fw[:, :, 128 * g:128 * g + 128]
        fox_in[g, :, :, 128:256] = fw[:, :, 1024 + 128 * g:1024 + 128 * g + 128]
        fox_in[g, :, :, 256:384] = fw[:, :, 2048 + 128 * g:2048 + 128 * g + 128]
    sh["fox_in"] = fox_in.reshape(8, 128, KC * 384)
    sh["fox_f"] = np.ascontiguousarray(fw[:, :, 3072:3088]).reshape(128, KC * 16)
    sh["fox_bf"] = np.ascontiguousarray(np.asarray(inp["fox_b_f"], f)[0].reshape(16, 1))
    sh["fox_out"] = np.ascontiguousarray(np.asarray(inp["fox_w_out"], f)[0].reshape(8, 128, D))
    gw = _kmajor(np.asarray(inp["gla_w_in"], f)[0])
    gla_in = np.empty((4, 128, KC, 768), f)
    for h in range(4):
        gla_in[h, :, :, 0:128] = gw[:, :, 128 * h:128 * h + 128]
        gla_in[h, :, :, 128:256] = gw[:, :, 512 + 128 * h:512 + 128 * h + 128]
        gla_in[h, :, :, 256:512] = gw[:, :, 1024 + 256 * h:1024 + 256 * h + 256]
        gla_in[h, :, :, 512:768] = gw[:, :, 2048 + 256 * h:2048 + 256 * h + 256]
    sh["gla_in"] = gla_in.reshape(4, 128, KC * 768)
    sh["gla_a"] = np.ascontiguousarray(gw[:, :, 3072:3088]).reshape(128, KC * 16)
    sh["gla_a2"] = np.ascontiguousarray(np.concatenate([np.asarray(inp["gla_w_a2"], f)[0], np.asarray(inp["gla_b_a"], f)[0][None, :]], axis=0))
    sh["gla_go"] = np.ascontiguousarray(np.asarray(inp["gla_g_o"], f)[0].reshape(KC, 128).T)
    go = np.asarray(inp["gla_w_out"], f)[0].reshape(4, 2, 128, D).transpose(0, 2, 1, 3)
    sh["gla_out"] = np.ascontiguousarray(go).reshape(4, 128, 2 * D)
    return sh


def prepare_core(x, c, b0, nseq, S):
    f = np.float32
    xs = np.asarray(x[b0:b0 + nseq], f)
    xT = np.ascontiguousarray(xs.reshape(nseq, S, KC, 128).transpose(0, 3, 2, 1))
    cs = np.asarray(c[b0:b0 + nseq], f)
    cT = np.ascontiguousarray(cs.reshape(nseq, KC, 128).transpose(2, 1, 0)).reshape(128, KC * nseq)
    return {"xT": xT, "cT": cT}


def unpack_out(o):
    nseq, _, _, S = o.shape
    return np.ascontiguousarray(o.transpose(0, 3, 2, 1)).reshape(nseq, S, D)


_CACHE = {}


def kernel(**inputs):
    x = np.asarray(inputs["x"])
    c = np.asarray(inputs["c"])
    B, S, _ = x.shape
    nseq = B // N_CORES
    key = (S, nseq)
    if key not in _CACHE:
        _CACHE[key] = build_program(S=S, nseq=nseq, depth=2)
    nc = _CACHE[key]
    sh = prepare_shared(inputs)
    in_maps = []
    for i in range(N_CORES):
        m = dict(sh)
        m.update(prepare_core(x, c, i * nseq, nseq, S))
        in_maps.append(m)
    res = run_bass_kernel_spmd(nc, in_maps, core_ids=list(range(N_CORES)))
    outs = [unpack_out(np.asarray(r["outT"])) for r in res.results]
    return np.concatenate(outs, axis=0).astype(np.float32)
```
